# Optimizing a Trainium2 kernel written in Bass

```python
import math
import jax
import jax.numpy as jnp
from jax import lax
import numpy as np

D_MODEL = 1024
BATCH = 4
SEQ = 4096
DEPTH = 4

N_MIXERS = 3
HEAD_DIM = 64
DIFF_HEADS = D_MODEL // (2 * HEAD_DIM)
ATTN_QBLOCK = 128
DIL_HEADS = D_MODEL // HEAD_DIM
DIL_PATTERNS = ((128, 1), (512, 4), (2048, 16))
DIL_QBLOCK = 64
RWKV_HEADS = D_MODEL // HEAD_DIM
DECAY_LORA = 64
AAA_LORA = 64
GATE_LORA = 128
MEM_LEN = 256
XATT_HEADS = 4
D_FF = 2816
CONV_WIDTH = 3
ALPHA = (2 * DEPTH) ** 0.25
BETA = (8 * DEPTH) ** -0.25
LN_EPS = 1e-5
GN_EPS = 64e-5
NEG_INF = -1e30
N_A = len(range(0, DEPTH, N_MIXERS))
N_B = len(range(1, DEPTH, N_MIXERS))
N_C = len(range(2, DEPTH, N_MIXERS))

kernel_name = "hybrid_diff_dilated_rwkv7_encoder"


def layer_norm(x, g, b):
    xf = x.astype(jnp.float32)
    mu = jnp.mean(xf, -1, keepdims=True)
    var = jnp.mean(jnp.square(xf - mu), -1, keepdims=True)
    return ((xf - mu) * lax.rsqrt(var + LN_EPS) * g + b).astype(x.dtype)


def rms_norm(x, g):
    xf = x.astype(jnp.float32)
    return (xf * lax.rsqrt(jnp.mean(jnp.square(xf), -1, keepdims=True) + LN_EPS) * g).astype(x.dtype)


def alibi_slopes(n):
    return jnp.exp2(-8.0 * jnp.arange(1, n + 1, dtype=jnp.float32) / n)


def diff_attention(x, w_qkv, lam, subln, w_o, lambda_init):
    B, S, D = x.shape
    H, E = DIFF_HEADS, HEAD_DIM
    q, k, v = jnp.split(x @ w_qkv, 3, axis=-1)
    q = q.reshape(B, S, H, 2, E)
    k = k.reshape(B, S, H, 2, E)
    v = v.reshape(B, S, H, 2 * E)
    lamf = lam.astype(jnp.float32)
    lam_full = jnp.exp(jnp.sum(lamf[0] * lamf[1])) - jnp.exp(jnp.sum(lamf[2] * lamf[3])) + lambda_init
    slopes = alibi_slopes(H)
    pos = jnp.arange(S)
    nblk = S // ATTN_QBLOCK
    qb = q.reshape(B, nblk, ATTN_QBLOCK, H, 2, E).transpose(1, 0, 2, 3, 4, 5)
    starts = jnp.arange(nblk) * ATTN_QBLOCK

    def block(args):
        q_blk, start = args
        s = jnp.einsum('bqhce,bkhce->bhcqk', q_blk, k,
                       preferred_element_type=jnp.float32) * (E ** -0.5)
        dist = jnp.abs((start + jnp.arange(ATTN_QBLOCK))[:, None] - pos[None, :]).astype(jnp.float32)
        s = s - slopes[None, :, None, None, None] * dist[None, None, None]
        p = jax.nn.softmax(s, axis=-1)
        attn = p[:, :, 0] - lam_full * p[:, :, 1]
        return jnp.einsum('bhqk,bkhe->bqhe', attn.astype(v.dtype), v)

    o = lax.map(block, (qb, starts))
    o = o.transpose(1, 0, 2, 3, 4).reshape(B, S, H, 2 * E)
    o = rms_norm(o, subln) * (1.0 - lambda_init)
    return o.reshape(B, S, D) @ w_o


def dilated_group(q, k, v, window, dilation, slopes):
    B, S, H, E = q.shape
    half = window // (2 * dilation)
    L = S // dilation
    nblk = -(-L // DIL_QBLOCK)
    Lp = nblk * DIL_QBLOCK
    KB = DIL_QBLOCK + 2 * half

    def to_res(a):
        return a.reshape(B, L, dilation, H, E).transpose(0, 2, 1, 3, 4)

    qr = jnp.pad(to_res(q), ((0, 0), (0, 0), (0, Lp - L), (0, 0), (0, 0)))
    pad_k = ((0, 0), (0, 0), (half, Lp - L + half), (0, 0), (0, 0))
    kr = jnp.pad(to_res(k), pad_k)
    vr = jnp.pad(to_res(v), pad_k)
    starts = jnp.arange(nblk) * DIL_QBLOCK
    idx = starts[:, None] + jnp.arange(KB)[None, :]
    kb = kr[:, :, idx]
    vb = vr[:, :, idx]
    qb = qr.reshape(B, dilation, nblk, DIL_QBLOCK, H, E)
    s = jnp.einsum('brnqhe,brnkhe->brnhqk', qb, kb,
                   preferred_element_type=jnp.float32) * (E ** -0.5)
    u_q = starts[:, None] + jnp.arange(DIL_QBLOCK)[None, :]
    u_k = idx - half
    rel = u_k[:, None, :] - u_q[:, :, None]
    valid = (jnp.abs(rel) <= half) & (u_k[:, None, :] >= 0) & (u_k[:, None, :] < L)
    dist = (dilation * jnp.abs(rel)).astype(jnp.float32)
    bias = -slopes[None, :, None, None] * dist[:, None]
    s = jnp.where(valid[:, None], s + bias, NEG_INF)
    lse = jax.nn.logsumexp(s, axis=-1)
    p = jnp.exp(s - lse[..., None])
    o = jnp.einsum('brnhqk,brnkhe->brnqhe', p.astype(v.dtype), vb)
    o = o.reshape(B, dilation, Lp, H, E)[:, :, :L].transpose(0, 2, 1, 3, 4).reshape(B, S, H, E)
    lse = lse.transpose(0, 1, 2, 4, 3).reshape(B, dilation, Lp, H)[:, :, :L]
    lse = lse.transpose(0, 2, 1, 3).reshape(B, S, H)
    return o, lse


def dilated_attention(x, w_qkv, w_o):
    B, S, D = x.shape
    H, E, G = DIL_HEADS, HEAD_DIM, len(DIL_PATTERNS)
    qkv = (x @ w_qkv).reshape(B, S, G, 3, H, E)
    slopes = alibi_slopes(H)
    outs, lses = [], []
    for g, (window, dilation) in enumerate(DIL_PATTERNS):
        o, l = dilated_group(qkv[:, :, g, 0], qkv[:, :, g, 1], qkv[:, :, g, 2], window, dilation, slopes)
        outs.append(o)
        lses.append(l)
    wts = jax.nn.softmax(jnp.stack(lses), axis=0)
    o = jnp.sum(wts[..., None] * jnp.stack(outs).astype(jnp.float32), axis=0)
    return o.astype(x.dtype).reshape(B, S, D) @ w_o


def token_shift(x, reverse):
    if reverse:
        return jnp.pad(x[:, 1:], ((0, 0), (0, 1), (0, 0)))
    return jnp.pad(x[:, :-1], ((0, 0), (1, 0), (0, 0)))


def wkv7_scan(r, w, k, v, kk, a, reverse):
    B, S, H, N = r.shape
    seq = tuple(t.transpose(1, 0, 2, 3) for t in (r, w, k, v, kk, a))

    def step(state, inp):
        r_t, w_t, k_t, v_t, kk_t, a_t = inp
        sa = jnp.einsum('bhij,bhj->bhi', state, -kk_t)
        state = (state * w_t[:, :, None, :] + sa[..., None] * (kk_t * a_t)[:, :, None, :]
                 + v_t[..., None] * k_t[:, :, None, :])
        return state, jnp.einsum('bhij,bhj->bhi', state, r_t)

    _, y = lax.scan(step, jnp.zeros((B, H, N, N), jnp.float32), seq, reverse=reverse)
    return y.transpose(1, 0, 2, 3)


def head_group_norm(y, g, b):
    H, N = y.shape[-2], y.shape[-1]
    mu = jnp.mean(y, -1, keepdims=True)
    var = jnp.mean(jnp.square(y - mu), -1, keepdims=True)
    return (y - mu) * lax.rsqrt(var + GN_EPS) * g.reshape(H, N) + b.reshape(H, N)


def rwkv7_direction(x, mu, w_r, w_k, w_v, w0, w1, w2, a0, a1, a2, g1, g2, k_k, k_a, r_k,
                    gn_g, gn_b, reverse):
    B, S, D = x.shape
    H, N = RWKV_HEADS, HEAD_DIM
    xx = token_shift(x, reverse) - x
    xr, xw, xk, xv, xa, xg = (x + xx * mu[j] for j in range(6))
    r = xr @ w_r
    k = xk @ w_k
    v = xv @ w_v
    w_log = -jax.nn.softplus(-(w0 + jnp.tanh(xw @ w1) @ w2).astype(jnp.float32)) - 0.5
    w = jnp.exp(-jnp.exp(w_log))
    a = jax.nn.sigmoid((a0 + (xa @ a1) @ a2).astype(jnp.float32))
    g = jax.nn.sigmoid(xg @ g1) @ g2

    def heads(t):
        return t.astype(jnp.float32).reshape(B, S, H, N)

    r, k, v, w, a = heads(r), heads(k), heads(v), heads(w), heads(a)
    kk = k * k_k.astype(jnp.float32).reshape(H, N)
    kk = kk / jnp.maximum(jnp.linalg.norm(kk, axis=-1, keepdims=True), 1e-12)
    k = k * (1.0 + (a - 1.0) * k_a.astype(jnp.float32).reshape(H, N))
    y = wkv7_scan(r, w, k, v, kk, a, reverse)
    y = head_group_norm(y, gn_g, gn_b) + jnp.sum(r * k * r_k.astype(jnp.float32), -1, keepdims=True) * v
    return (y.reshape(B, S, D) * g).astype(x.dtype)


def rwkv7_time_mix(x, mu, w_rkv, w0, w1, w2, a0, a1, a2, g1, g2, k_k, k_a, r_k, gn_g, gn_b, w_o):
    fwd = rwkv7_direction(x, mu[0], w_rkv[0], w_rkv[1], w_rkv[2], w0[0], w1[0], w2[0], a0[0], a1[0], a2[0],
                          g1[0], g2[0], k_k[0], k_a[0], r_k, gn_g, gn_b, False)
    bwd = rwkv7_direction(x, mu[1], w_rkv[0], w_rkv[1], w_rkv[2], w0[1], w1[1], w2[1], a0[1], a1[1], a2[1],
                          g1[1], g2[1], k_k[1], k_a[1], r_k, gn_g, gn_b, True)
    return (fwd + bwd) @ w_o


def memory_cross_attention(x, mem, w_q, w_kv, w_o):
    B, S, D = x.shape
    M = mem.shape[1]
    H = XATT_HEADS
    E = D // H
    q = (x @ w_q).reshape(B, S, H, E)
    kv = (mem @ w_kv).reshape(B, M, 2, H, E)
    s = jnp.einsum('bshe,bmhe->bhsm', q, kv[:, :, 0], preferred_element_type=jnp.float32) * (E ** -0.5)
    p = jax.nn.softmax(s, axis=-1).astype(x.dtype)
    o = jnp.einsum('bhsm,bmhe->bshe', p, kv[:, :, 1])
    return o.reshape(B, S, D) @ w_o


def conv_glu(x, w_in, conv_w, conv_b, w_out):
    gate, val = jnp.split(x @ w_in, 2, axis=-1)
    gate = lax.conv_general_dilated(gate, conv_w[:, None, :], window_strides=(1,),
                                    padding=((CONV_WIDTH // 2, CONV_WIDTH // 2),),
                                    dimension_numbers=('NWC', 'WIO', 'NWC'),
                                    feature_group_count=gate.shape[-1]) + conv_b
    return (jax.nn.gelu(gate, approximate=False) * val) @ w_out


def setup_inputs(seed: int = 0) -> dict:
    key = jax.random.key(seed)
    keys = iter(jax.random.split(key, 48))
    D, F, H, N = D_MODEL, D_FF, RWKV_HEADS, HEAD_DIM
    G = len(DIL_PATTERNS)
    inv = D ** -0.5

    def normal(shape, scale):
        return jax.random.normal(next(keys), shape, jnp.float32) * scale

    def uniform(shape, lo, hi):
        return jax.random.uniform(next(keys), shape, jnp.float32, lo, hi)

    return {
        "x": normal((BATCH, SEQ, D), 1.0),
        "mem": normal((BATCH, MEM_LEN, D), 1.0),
        "diff_w_qkv": normal((N_A, D, 3 * D), inv),
        "diff_lambda": normal((N_A, 4, HEAD_DIM), 0.1),
        "diff_subln": 1.0 + normal((N_A, 2 * HEAD_DIM), 0.02),
        "diff_w_o": normal((N_A, D, D), inv * BETA),
        "dil_w_qkv": normal((N_B, D, G * 3 * D), inv),
        "dil_w_o": normal((N_B, D, D), inv * BETA),
        "rwkv_mu": uniform((N_C, 2, 6, D), 0.0, 1.0),
        "rwkv_w_rkv": normal((N_C, 3, D, D), inv),
        "rwkv_w0": uniform((N_C, 2, D), -5.0, 0.0),
        "rwkv_w1": normal((N_C, 2, D, DECAY_LORA), inv),
        "rwkv_w2": normal((N_C, 2, DECAY_LORA, D), 0.1 * DECAY_LORA ** -0.5),
        "rwkv_a0": normal((N_C, 2, D), 0.1),
        "rwkv_a1": normal((N_C, 2, D, AAA_LORA), inv),
        "rwkv_a2": normal((N_C, 2, AAA_LORA, D), 0.1 * AAA_LORA ** -0.5),
        "rwkv_g1": normal((N_C, 2, D, GATE_LORA), inv),
        "rwkv_g2": normal((N_C, 2, GATE_LORA, D), GATE_LORA ** -0.5),
        "rwkv_k_k": 0.85 + normal((N_C, 2, D), 0.02),
        "rwkv_k_a": 1.0 + normal((N_C, 2, D), 0.02),
        "rwkv_r_k": normal((N_C, H, N), 0.1),
        "rwkv_gn_g": 1.0 + normal((N_C, D), 0.02),
        "rwkv_gn_b": normal((N_C, D), 0.02),
        "rwkv_w_o": normal((N_C, D, D), inv * BETA),
        "xatt_w_q": normal((DEPTH, D, D), inv),
        "xatt_w_kv": normal((DEPTH, D, 2 * D), inv),
        "xatt_w_o": normal((DEPTH, D, D), inv * BETA),
        "ffn_w_in": normal((DEPTH, D, 2 * F), inv),
        "ffn_conv_w": normal((DEPTH, CONV_WIDTH, F), CONV_WIDTH ** -0.5),
        "ffn_conv_b": normal((DEPTH, F), 0.02),
        "ffn_w_out": normal((DEPTH, F, D), F ** -0.5 * BETA),
        "ln_g": 1.0 + normal((DEPTH, 3, D), 0.02),
        "ln_b": normal((DEPTH, 3, D), 0.02),
    }


def reference(x, mem, diff_w_qkv, diff_lambda, diff_subln, diff_w_o, dil_w_qkv, dil_w_o,
              rwkv_mu, rwkv_w_rkv, rwkv_w0, rwkv_w1, rwkv_w2, rwkv_a0, rwkv_a1, rwkv_a2,
              rwkv_g1, rwkv_g2, rwkv_k_k, rwkv_k_a, rwkv_r_k, rwkv_gn_g, rwkv_gn_b, rwkv_w_o,
              xatt_w_q, xatt_w_kv, xatt_w_o, ffn_w_in, ffn_conv_w, ffn_conv_b, ffn_w_out,
              ln_g, ln_b):
    for i in range(DEPTH):
        m, j = i % N_MIXERS, i // N_MIXERS
        if m == 0:
            lambda_init = 0.8 - 0.6 * math.exp(-0.3 * i)
            h = diff_attention(x, diff_w_qkv[j], diff_lambda[j], diff_subln[j], diff_w_o[j], lambda_init)
        elif m == 1:
            h = dilated_attention(x, dil_w_qkv[j], dil_w_o[j])
        else:
            h = rwkv7_time_mix(x, rwkv_mu[j], rwkv_w_rkv[j], rwkv_w0[j], rwkv_w1[j], rwkv_w2[j],
                               rwkv_a0[j], rwkv_a1[j], rwkv_a2[j], rwkv_g1[j], rwkv_g2[j],
                               rwkv_k_k[j], rwkv_k_a[j], rwkv_r_k[j], rwkv_gn_g[j], rwkv_gn_b[j],
                               rwkv_w_o[j])
        x = layer_norm(ALPHA * x + h, ln_g[i, 0], ln_b[i, 0])
        x = layer_norm(ALPHA * x + memory_cross_attention(x, mem, xatt_w_q[i], xatt_w_kv[i], xatt_w_o[i]),
                       ln_g[i, 1], ln_b[i, 1])
        x = layer_norm(ALPHA * x + conv_glu(x, ffn_w_in[i], ffn_conv_w[i], ffn_conv_b[i], ffn_w_out[i]),
                       ln_g[i, 2], ln_b[i, 2])
    return x
```

```python
import math, os
import numpy as np
from contextlib import ExitStack
import concourse.bass as bass
import concourse.mybir as mybir

F32 = mybir.dt.float32
BF16 = mybir.dt.bfloat16
ALU = mybir.AluOpType
AF = mybir.ActivationFunctionType
AX = mybir.AxisListType


class View:
    __slots__ = ("buf", "ap")

    def __init__(self, buf, ap):
        self.buf = buf
        self.ap = ap

    def __getitem__(self, k):
        return View(self.buf, self.ap[k])

    def rearrange(self, s, **kw):
        return View(self.buf, self.ap.rearrange(s, **kw))

    def bitcast(self, dt):
        return View(self.buf, self.ap.bitcast(dt))

    def to_broadcast(self, shape):
        return View(self.buf, self.ap.to_broadcast(shape))

    def broadcast_to(self, shape):
        return View(self.buf, self.ap.broadcast_to(shape))

    def partition_broadcast(self, n):
        return View(self.buf, self.ap.partition_broadcast(n))

    def unsqueeze(self, a):
        return View(self.buf, self.ap.unsqueeze(a))

    @property
    def shape(self):
        return self.ap.shape


class Buf:
    __slots__ = ("ap", "w", "r", "disjoint", "name")

    def __init__(self, ap, disjoint=False, name=""):
        self.ap = ap
        self.w = {}
        self.r = {}
        self.disjoint = disjoint
        self.name = name

    def __getitem__(self, k):
        return View(self, self.ap[k])

    @property
    def v(self):
        return View(self, self.ap)

    def rearrange(self, s, **kw):
        return View(self, self.ap.rearrange(s, **kw))


class Eng:
    def __init__(self, name, e, sem, same_sync):
        self.name = name
        self.e = e
        self.sem = sem
        self.count = 0
        self.seen = {}
        self.same_sync = same_sync
        self.nwaits = 0

    def wait(self, tok):
        sem, val = tok
        if sem is self.sem and not self.same_sync:
            return
        k = id(sem)
        if self.seen.get(k, 0) >= val:
            return
        self.e.wait_ge(sem, val)
        self.nwaits += 1
        self.seen[k] = val


def _merge(d, sem, val):
    k = id(sem)
    if k not in d or d[k][1] < val:
        d[k] = (sem, val)


class Prog:
    def __init__(self, n_dma_sems=8):
        self.nc = nc = bass.Bass("TRN2", target_bir_lowering=False)
        self.es = ExitStack()
        self.engs = {}
        import os
        allss = os.environ.get("ALLSS", "0") == "1"
        for name, e, ss in (("pe", nc.tensor, False), ("act", nc.scalar, True), ("dve", nc.vector, True),
                            ("pool", nc.gpsimd, True), ("sp", nc.sync, False)):
            sem = self.es.enter_context(nc.semaphore("s_" + name))
            self.engs[name] = Eng(name, e, sem, ss)
        self.dsems = {}
        for q in ("sp", "pool", "act"):
            self.dsems[q] = [[self.es.enter_context(nc.semaphore(f"d_{q}{i}")), 0] for i in range(n_dma_sems)]
        self.dnext = {"sp": 0, "pool": 0, "act": 0}
        self.stack = self.es
        self.uid = 0
        self.psum = []
        for i in range(8):
            h = self.es.enter_context(nc.psum_tensor(f"ps{i}", [128, 512], F32))
            self.psum.append(Buf(h[:], name=f"ps{i}"))
        self.ndma = 0

    def sb(self, shape, dtype=F32, name=None):
        self.uid += 1
        name = f"{name or 't'}_{self.uid}"
        h = self.stack.enter_context(self.nc.sbuf_tensor(name, list(shape), dtype))
        return Buf(h[:], name=name)

    def dram(self, name, shape, dtype=F32, kind="Internal", disjoint=True):
        t = self.nc.dram_tensor(name, list(shape), dtype, kind=kind)
        return Buf(t.ap(), disjoint=disjoint, name=name)

    class _Phase:
        def __init__(self, p):
            self.p = p

        def __enter__(self):
            self.prev = self.p.stack
            self.p.stack = ExitStack()
            return self

        def __exit__(self, *a):
            self.p.barrier()
            self.p.stack.close()
            self.p.stack = self.prev
            return False

    def phase(self):
        return Prog._Phase(self)

    def _deps(self, eng, reads, writes):
        for v in reads:
            b = v.buf if isinstance(v, View) else v
            for tok in b.w.values():
                eng.wait(tok)
        for v in writes:
            b = v.buf if isinstance(v, View) else v
            if not b.disjoint:
                for tok in b.w.values():
                    eng.wait(tok)
            for tok in b.r.values():
                eng.wait(tok)

    def _commit(self, sem, val, reads, writes):
        for v in reads:
            b = v.buf if isinstance(v, View) else v
            _merge(b.r, sem, val)
        for v in writes:
            b = v.buf if isinstance(v, View) else v
            if b.disjoint:
                _merge(b.w, sem, val)
            else:
                b.w = {id(sem): (sem, val)}
                b.r = {}

    def run(self, engname, fn, reads, writes):
        eng = self.engs[engname]
        self._deps(eng, reads, writes)
        ins = fn(eng.e)
        eng.count += 1
        ins.then_inc(eng.sem, 1)
        self._commit(eng.sem, eng.count, reads, writes)
        return ins

    def dma(self, q, out, in_, **kw):
        eng = self.engs[q]
        self._deps(eng, [in_], [out])
        slots = self.dsems[q]
        i = self.dnext[q]
        self.dnext[q] = (i + 1) % len(slots)
        sem, val = slots[i]
        if val > 0:
            eng.wait((sem, val))
        eng.e.dma_start(out=out.ap, in_=in_.ap, **kw).then_inc(sem, 16)
        val += 16
        slots[i][1] = val
        self._commit(sem, val, [in_], [out])
        self.ndma += 1

    def barrier(self):
        toks = [(e.sem, e.count) for e in self.engs.values() if e.count > 0]
        for q in self.dsems:
            toks += [(s, v) for s, v in self.dsems[q] if v > 0]
        for e in self.engs.values():
            for t in toks:
                e.wait(t)

    def finish(self):
        self.barrier()
        self.es.close()

    @staticmethod
    def _a(x):
        return x.ap if isinstance(x, View) else x

    @staticmethod
    def _vs(*xs):
        return [x for x in xs if isinstance(x, View)]

    def mm(self, out, lhsT, rhs, start=True, stop=True, **kw):
        return self.run("pe", lambda e: e.matmul(out.ap, lhsT.ap, rhs.ap, start=start, stop=stop, **kw),
                        [lhsT, rhs], [out])

    def transpose(self, out, in_, ident):
        return self.run("pe", lambda e: e.transpose(out.ap, in_.ap, ident.ap), [in_, ident], [out])

    def act(self, out, in_, func, bias=None, scale=1.0, accum_out=None, eng="act"):
        kw = {}
        if bias is not None:
            kw["bias"] = self._a(bias)
        if accum_out is not None:
            kw["accum_out"] = accum_out.ap
        sc = self._a(scale)
        w = [out] + ([accum_out] if accum_out is not None else [])
        return self.run(eng, lambda e: e.activation(out.ap, in_.ap, func, scale=sc, **kw),
                        self._vs(in_, bias, scale), w)

    def tt(self, eng, out, in0, in1, op):
        return self.run(eng, lambda e: e.tensor_tensor(out.ap, in0.ap, in1.ap, op), [in0, in1], [out])

    def ts(self, eng, out, in0, s1, s2, op0, op1=None, accum_out=None):
        kw = {}
        if op1 is not None:
            kw["op1"] = op1
        if accum_out is not None:
            kw["accum_out"] = accum_out.ap
        w = [out] + ([accum_out] if accum_out is not None else [])
        return self.run(eng, lambda e: e.tensor_scalar(out.ap, in0.ap, self._a(s1), self._a(s2) if s2 is not None else None,
                                                       op0, **kw),
                        self._vs(in0, s1, s2), w)

    def stt(self, eng, out, in0, scalar, in1, op0, op1):
        return self.run(eng, lambda e: e.scalar_tensor_tensor(out.ap, in0.ap, self._a(scalar), in1.ap, op0, op1),
                        self._vs(in0, scalar, in1), [out])

    def copy(self, eng, out, in_):
        if eng == "act":
            return self.run(eng, lambda e: e.copy(out.ap, in_.ap), [in_], [out])
        return self.run(eng, lambda e: e.tensor_copy(out.ap, in_.ap), [in_], [out])

    def memset(self, eng, out, val):
        return self.run(eng, lambda e: e.memset(out.ap, val), [], [out])

    def reduce(self, eng, out, in_, op, axis=AX.X):
        return self.run(eng, lambda e: e.tensor_reduce(out.ap, in_.ap, axis, op), [in_], [out])

    def recip(self, out, in_):
        return self.run("dve", lambda e: e.reciprocal(out.ap, in_.ap), [in_], [out])

    def bn_stats(self, out, in_):
        return self.run("dve", lambda e: e.bn_stats(out.ap, in_.ap), [in_], [out])

    def bn_aggr(self, out, in_):
        return self.run("dve", lambda e: e.bn_aggr(out.ap, in_.ap), [in_], [out])


D = 1024
S = 4096
NB = 4
KC = 8
PAD = 128
SP_ = S + 2 * PAD
NT = 2048
DFF = 2816
FC = 22
ALPHA = 8 ** 0.25
LN_EPS = 1e-5
GN_EPS = 64e-5
MEM = 256


class Ctx:
    pass


def setup_common(P, ident_dram):
    c = Ctx()
    c.ident = P.sb([128, 128], F32, "ident")
    P.dma("sp", c.ident.v, ident_dram.v)
    c.rr = 0
    return c


def evac_engine(c):
    c.rr += 1
    return "act" if (c.rr & 1) else "dve"


def load_xT(P, c, x_dram_rows, ntiles, xT, col0=0, xstage=None, keep=None):
    if xstage is None:
        xstage = [P.sb([128, D], F32, "xstage") for _ in range(2)]
    for t in range(ntiles):
        xs = keep[t] if keep is not None else xstage[t % 2]
        P.dma("sp" if t % 2 == 0 else "pool", xs.v, x_dram_rows[t * 128:(t + 1) * 128, :])
        transpose_tile(P, c, xs, xT, col0 + t * 128)


def transpose_tile(P, c, xs, xT, col, ncols=128, rows=128):
    for half in range(2):
        ps = P.psum[c.tp_banks[c.tp_i % len(c.tp_banks)]]
        c.tp_i += 1
        for j in range(4):
            k = half * 4 + j
            P.transpose(ps[:, j * 128:j * 128 + rows], xs[0:rows, k * 128:(k + 1) * 128], c.ident[0:rows, 0:rows])
        src = ps.rearrange("p (j n) -> p j n", j=4)[:, :, 0:rows]
        P.copy(evac_engine(c), xT[:, half * 4:half * 4 + 4, col:col + rows], src)


def load_w_bf16(P, c, dst, src_rows, kc, ncols, q="pool", conv_eng="pool"):
    per = max(1, 2048 // ncols)
    k = 0
    while k < kc:
        n = min(per, kc - k)
        st = c.wstage[c.ws_i % len(c.wstage)]
        c.ws_i += 1
        sv = st[:, 0:n * ncols].rearrange("p (k n) -> p k n", k=n)
        P.dma(q, sv, src_rows[k * 128:(k + n) * 128, :].rearrange("(k p) n -> p k n", p=128))
        P.copy(conv_eng, dst[:, k:k + n, :], sv)
        k += n


def ln_setup(P, c, g_dram, b_dram):
    g = P.sb([128, D], F32, "ln_g")
    b = P.sb([128, D], F32, "ln_b")
    P.dma("sp", g.v, g_dram.partition_broadcast(128))
    P.dma("sp", b.v, b_dram.partition_broadcast(128))
    return g, b


def ln_epilogue(P, c, ps_chunks, xres, g, b, out_dram_rows, lnbufs):
    t, y, st, mv, rs, nm = lnbufs
    for cc in range(2):
        P.stt("dve", t[:, cc * 512:(cc + 1) * 512], xres[:, cc * 512:(cc + 1) * 512], ALPHA, ps_chunks[cc],
              ALU.mult, ALU.add)
        P.bn_stats(st[:, cc, :], t[:, cc * 512:(cc + 1) * 512])
    P.bn_aggr(mv.v, st.v)
    P.act(rs.v, mv[:, 1:2], AF.Sqrt, bias=LN_EPS)
    P.recip(rs.v, rs.v)
    P.stt("dve", nm.v, mv[:, 0:1], -1.0, rs.v, ALU.mult, ALU.mult)
    P.act(y.v, t.v, AF.Identity, bias=nm.v, scale=rs.v)
    P.tt("pool", y.v, y.v, g.v, ALU.mult)
    P.tt("pool", y.v, y.v, b.v, ALU.add)
    P.dma("sp", out_dram_rows, y.v)


def ln_bufs(P):
    return (P.sb([128, D], F32, "ln_t"), P.sb([128, D], F32, "ln_y"), P.sb([128, 2, 6], F32, "ln_st"),
            P.sb([128, 2], F32, "ln_mv"), P.sb([128, 1], F32, "ln_rs"), P.sb([128, 1], F32, "ln_nm"))


def ffn_phase(P, c, xpad, row0, out_rows, w_in, w_out, convw, convb, g_dram, b_dram):
    with P.phase():
        c.tp_banks = [6, 7]
        c.tp_i = 0
        c.wstage = [P.sb([128, 2048], F32, "wstage") for _ in range(2)]
        c.ws_i = 0
        HT = 1024
        xT = P.sb([128, KC, HT + 128], BF16, "xT")
        hT = P.sb([128, FC, HT], BF16, "hT")
        wo = P.sb([128, FC, D], BF16, "wo")
        wg = [P.sb([128, KC, 128], BF16, "wg") for _ in range(2)]
        wv = [P.sb([128, KC, 128], BF16, "wv") for _ in range(2)]
        cw = P.sb([128, FC, 3], F32, "cw")
        cb = P.sb([128, FC], F32, "cb")
        cbuf = [P.sb([128, 256], F32, "cbuf") for _ in range(2)]
        gbuf = [P.sb([128, 256], F32, "gbuf") for _ in range(2)]
        xres = [P.sb([128, D], F32, "xres") for _ in range(2)]
        lnb = [ln_bufs(P) for _ in range(2)]
        g, b = ln_setup(P, c, g_dram, b_dram)
        P.dma("sp", cw.v, convw.v)
        P.dma("sp", cb.v, convb.v)
        load_w_bf16(P, c, wo.v, w_out.v, FC, D, q="pool", conv_eng="pool")
        xstage = [P.sb([128, D], F32, "xstage") for _ in range(2)]
        for half in range(NT // HT):
            r0 = row0 + half * HT - 1
            load_xT(P, c, xpad[r0:r0 + HT + 128, :], (HT + 128) // 128, xT, 0, xstage=xstage)
            for j in range(FC):
                load_w_bf16(P, c, wg[j % 2].v, w_in[:, j * 128:(j + 1) * 128], KC, 128)
                load_w_bf16(P, c, wv[j % 2].v, w_in[:, DFF + j * 128:DFF + (j + 1) * 128], KC, 128)
                for tb in range(HT // 256):
                    c0 = tb * 256
                    pg = P.psum[(2 * tb) % 4]
                    pv = P.psum[(2 * tb + 1) % 4]
                    for k in range(KC):
                        P.mm(pg[:, 0:258], wg[j % 2][:, k, :], xT[:, k, c0:c0 + 258], start=(k == 0), stop=(k == KC - 1))
                    for k in range(KC):
                        P.mm(pv[:, 0:256], wv[j % 2][:, k, :], xT[:, k, c0 + 1:c0 + 257], start=(k == 0),
                             stop=(k == KC - 1))
                    cbf = cbuf[tb % 2]
                    gb = gbuf[tb % 2]
                    P.ts("dve", cbf.v, pg[:, 1:257], cw[:, j, 1:2], cb[:, j:j + 1], ALU.mult, ALU.add)
                    P.stt("dve", cbf.v, pg[:, 0:256], cw[:, j, 0:1], cbf.v, ALU.mult, ALU.add)
                    P.stt("dve", cbf.v, pg[:, 2:258], cw[:, j, 2:3], cbf.v, ALU.mult, ALU.add)
                    P.act(gb.v, cbf.v, AF.Gelu)
                    P.tt("dve", hT[:, j, c0:c0 + 256], gb.v, pv[:, 0:256], ALU.mult)
            for tt in range(HT // 128):
                xr = xres[tt % 2]
                rr = row0 + half * HT + tt * 128
                P.dma("pool", xr.v, xpad[rr:rr + 128, :])
                pcs = []
                for cc in range(2):
                    ps = P.psum[4 + cc]
                    for j in range(FC):
                        P.mm(ps.v, hT[:, j, tt * 128:(tt + 1) * 128], wo[:, j, cc * 512:(cc + 1) * 512],
                             start=(j == 0), stop=(j == FC - 1))
                    pcs.append(ps.v)
                o0 = half * HT + tt * 128
                ln_epilogue(P, c, pcs, xr.v, g, b, out_rows[o0:o0 + 128, :], lnb[tt % 2])


def proj_ln_phase(P, c, src_rows, res_rows, w, g_dram, b_dram, out_rows, ntok=NT, src2_rows=None):
    with P.phase():
        c.tp_banks = [6, 7]
        c.tp_i = 0
        c.wstage = [P.sb([128, 2048], F32, "wstage") for _ in range(2)]
        c.ws_i = 0
        wo = P.sb([128, KC, D], BF16, "wo")
        load_w_bf16(P, c, wo.v, w.v, KC, D)
        g, b = ln_setup(P, c, g_dram, b_dram)
        osb = [P.sb([128, D], F32, "osb") for _ in range(2)]
        osb2 = [P.sb([128, D], F32, "osb2") for _ in range(2)] if src2_rows is not None else None
        xres = [P.sb([128, D], F32, "xres") for _ in range(2)]
        oT = [P.sb([128, KC, 128], BF16, "oT") for _ in range(2)]
        lnb = [ln_bufs(P) for _ in range(2)]
        for t in range(ntok // 128):
            o = osb[t % 2]
            P.dma("sp", o.v, src_rows[t * 128:(t + 1) * 128, :])
            if src2_rows is not None:
                o2 = osb2[t % 2]
                P.dma("sp", o2.v, src2_rows[t * 128:(t + 1) * 128, :])
                P.tt("pool", o.v, o.v, o2.v, ALU.add)
            P.dma("pool", xres[t % 2].v, res_rows[t * 128:(t + 1) * 128, :])
            transpose_tile(P, c, o, oT[t % 2], 0)
            pcs = []
            for cc in range(2):
                ps = P.psum[(t % 2) * 2 + cc]
                for k in range(KC):
                    P.mm(ps.v, oT[t % 2][:, k, :], wo[:, k, cc * 512:(cc + 1) * 512], start=(k == 0), stop=(k == KC - 1))
                pcs.append(ps.v)
            ln_epilogue(P, c, pcs, xres[t % 2].v, g, b, out_rows[t * 128:(t + 1) * 128, :], lnb[t % 2])


def xatt_phase(P, c, x1_rows, out_rows, mem, w_q, w_kv, w_o, g_dram, b_dram):
    with P.phase():
        c.tp_banks = [6, 7]
        c.tp_i = 0
        c.wstage = [P.sb([128, 2048], F32, "wstage") for _ in range(2)]
        c.ws_i = 0
        memT = P.sb([128, KC, MEM], BF16, "memT")
        load_xT(P, c, mem.v, 2, memT)
        wq = P.sb([128, KC, D], BF16, "wq")
        wo = P.sb([128, KC, D], BF16, "wo")
        wk = P.sb([128, KC, 512], BF16, "wk")
        kT = P.sb([128, 8, MEM], BF16, "kT")
        va = P.sb([128, 2, 4, 258], BF16, "va")
        for blk in range(2):
            load_w_bf16(P, c, wk.v, w_kv[:, blk * 512:(blk + 1) * 512], KC, 512)
            for f4 in range(4):
                f = blk * 4 + f4
                ps = P.psum[f % 4]
                for k in range(KC):
                    P.mm(ps[:, 0:MEM], wk[:, k, f4 * 128:(f4 + 1) * 128], memT[:, k, :], start=(k == 0), stop=(k == KC - 1))
                P.copy(evac_engine(c), kT[:, f, :], ps[:, 0:MEM])
        for blk in range(2):
            load_w_bf16(P, c, wk.v, w_kv[:, D + blk * 512:D + (blk + 1) * 512], KC, 512)
            for mt in range(2):
                ps = P.psum[(blk * 2 + mt) % 4]
                for k in range(KC):
                    P.mm(ps.v, memT[:, k, mt * 128:(mt + 1) * 128], wk[:, k, :], start=(k == 0), stop=(k == KC - 1))
                P.copy(evac_engine(c), va[:, mt, 2 * blk:2 * blk + 2, 0:256], ps.rearrange("p (h e) -> p h e", h=2))
        P.memset("pool", va[:, :, :, 256:258], 1.0)
        load_w_bf16(P, c, wq.v, w_q.v, KC, D)
        load_w_bf16(P, c, wo.v, w_o.v, KC, D)
        g, b = ln_setup(P, c, g_dram, b_dram)
        xk = [P.sb([128, D], F32, "xk") for _ in range(4)]
        osb = [P.sb([128, D], F32, "osb") for _ in range(4)]
        xT = P.sb([128, KC, 512], BF16, "xT")
        qT = P.sb([128, KC, 512], BF16, "qT")
        PT = [P.sb([128, 512], BF16, "PT") for _ in range(2)]
        oT = [P.sb([128, KC, 128], BF16, "oT") for _ in range(2)]
        rc = [P.sb([128, 1], F32, "rc") for _ in range(2)]
        lnb = [ln_bufs(P) for _ in range(2)]
        for grp in range(NT // 512):
            load_xT(P, c, x1_rows[grp * 512:(grp + 1) * 512, :], 4, xT, keep=xk)
            for f in range(8):
                ps = P.psum[f % 2]
                for k in range(KC):
                    P.mm(ps.v, wq[:, k, f * 128:(f + 1) * 128], xT[:, k, :], start=(k == 0), stop=(k == KC - 1))
                P.act(qT[:, f, :], ps.v, AF.Identity, scale=1.0 / 16.0)
            for h in range(4):
                for mt in range(2):
                    ps = P.psum[2 + mt]
                    for e in range(2):
                        P.mm(ps.v, kT[:, 2 * h + e, mt * 128:(mt + 1) * 128], qT[:, 2 * h + e, :], start=(e == 0), stop=(e == 1))
                    P.act(PT[mt].v, ps.v, AF.Exp)
                for tt in range(4):
                    po = P.psum[4 + tt % 2]
                    for mt in range(2):
                        P.mm(po[:, 0:257], PT[mt][:, tt * 128:(tt + 1) * 128], va[:, mt, h, 0:257], start=(mt == 0), stop=(mt == 1))
                    r = rc[tt % 2]
                    P.recip(r.v, po[:, 256:257])
                    P.ts("dve", osb[tt][:, h * 256:(h + 1) * 256], po[:, 0:256], r.v, None, ALU.mult)
            for tt in range(4):
                transpose_tile(P, c, osb[tt], oT[tt % 2], 0)
                pcs = []
                for cc in range(2):
                    ps = P.psum[cc]
                    for k in range(KC):
                        P.mm(ps.v, oT[tt % 2][:, k, :], wo[:, k, cc * 512:(cc + 1) * 512], start=(k == 0), stop=(k == KC - 1))
                    pcs.append(ps.v)
                r0 = grp * 512 + tt * 128
                ln_epilogue(P, c, pcs, xk[tt].v, g, b, out_rows[r0:r0 + 128, :], lnb[tt % 2])


def diff_phase(P, c, xfull_rows, xown_rows, w_qkv, lam_dram, subln_dram, dtab_dram, linit_dram, o_s, scr):
    kTs, vs, qTs = scr["kTs"], scr["vs"], scr["qTs"]
    with P.phase():
        c.tp_banks = [6, 7]
        c.tp_i = 0
        c.wstage = [P.sb([128, 2048], F32, "wstage") for _ in range(2)]
        c.ws_i = 0
        xT = P.sb([128, KC, S], BF16, "xTfull")
        xstage = [P.sb([128, D], F32, "xstage") for _ in range(2)]
        load_xT(P, c, xfull_rows, S // 128, xT, xstage=xstage)
        wblk = P.sb([128, KC, 512], BF16, "wblk")
        with P.phase():
            vsb = P.sb([128, 8, 32, 130], BF16, "vsb")
            P.memset("pool", vsb[:, :, :, 128:130], 1.0)
            for blk in range(2):
                load_w_bf16(P, c, wblk.v, w_qkv[:, 2 * D + blk * 512:2 * D + (blk + 1) * 512], KC, 512)
                for t in range(S // 128):
                    ps = P.psum[t % 4]
                    for k in range(KC):
                        P.mm(ps.v, xT[:, k, t * 128:(t + 1) * 128], wblk[:, k, :], start=(k == 0), stop=(k == KC - 1))
                    P.copy(evac_engine(c), vsb[:, 4 * blk:4 * blk + 4, t, 0:128], ps.rearrange("p (h e) -> p h e", h=4))
            for h in range(8):
                P.dma("sp" if h % 2 else "pool", vs[h], vsb[:, h, :, :].rearrange("p t e -> p (t e)"))
        with P.phase():
            xTo = P.sb([128, KC, NT], BF16, "xTown")
            load_xT(P, c, xown_rows, NT // 128, xTo, xstage=xstage)
            kh = [P.sb([128, S], BF16, "kh") for _ in range(2)]
            i = 0
            for which, base, ntok, src, dst, scale in (("k", D, S, xT, kTs, 1.0), ("q", 0, NT, xTo, qTs, 0.125)):
                for blk in range(2):
                    load_w_bf16(P, c, wblk.v, w_qkv[:, base + blk * 512:base + (blk + 1) * 512], KC, 512)
                    for h4 in range(4):
                        h = blk * 4 + h4
                        kb = kh[i % 2]
                        i += 1
                        for tc in range(ntok // 512):
                            ps = P.psum[tc % 4]
                            for k in range(KC):
                                P.mm(ps.v, wblk[:, k, h4 * 128:(h4 + 1) * 128], src[:, k, tc * 512:(tc + 1) * 512],
                                     start=(k == 0), stop=(k == KC - 1))
                            if scale == 1.0:
                                P.copy(evac_engine(c), kb[:, tc * 512:(tc + 1) * 512], ps.v)
                            else:
                                P.act(kb[:, tc * 512:(tc + 1) * 512], ps.v, AF.Identity, scale=scale)
                        P.dma("sp" if h % 2 else "pool", dst[h], kb[:, 0:ntok])
    with P.phase():
        dtab = P.sb([128, 8192], F32, "dtab")
        P.dma("sp", dtab[:, 0:4096], dtab_dram[:, 0:4096])
        P.dma("pool", dtab[:, 4096:8192], dtab_dram[:, 4096:8192])
        lamb = P.sb([128, 256], F32, "lamb")
        P.dma("sp", lamb.v, lam_dram.partition_broadcast(128))
        pr = P.sb([128, 2, 64], F32, "pr")
        s12 = P.sb([128, 2], F32, "s12")
        lb3 = lamb.rearrange("p (a e) -> p a e", a=4)
        P.tt("dve", pr[:, 0, :], lb3[:, 0, :], lb3[:, 1, :], ALU.mult)
        P.tt("dve", pr[:, 1, :], lb3[:, 2, :], lb3[:, 3, :], ALU.mult)
        P.reduce("dve", s12.v, pr.v, ALU.add)
        P.act(s12.v, s12.v, AF.Exp)
        nlam = P.sb([128, 1], F32, "nlam")
        lin = P.sb([128, 2], F32, "lin")
        P.dma("sp", lin.v, linit_dram.partition_broadcast(128))
        P.tt("dve", nlam.v, s12[:, 1:2], s12[:, 0:1], ALU.subtract)
        P.tt("dve", nlam.v, nlam.v, lin[:, 0:1], ALU.add)
        sl = P.sb([128, 128], F32, "sl")
        P.dma("sp", sl.v, subln_dram.partition_broadcast(128))
        P.ts("dve", sl.v, sl.v, lin[:, 1:2], None, ALU.mult)
        kbuf = [P.sb([128, S], BF16, "kbuf") for _ in range(2)]
        vbuf = [P.sb([128, 32, 130], BF16, "vbuf") for _ in range(2)]
        qbuf = [P.sb([128, NT], BF16, "qbuf") for _ in range(2)]
        tmpb = [P.sb([128, 512], F32, "tmpb") for _ in range(3)]
        ptb = [P.sb([128, 512], BF16, "ptb") for _ in range(3)]
        oc = P.sb([128, 2, 16, 128], F32, "oc")
        rc = [P.sb([128, 1], F32, "rc") for _ in range(2)]
        od = [P.sb([128, 128], F32, "od") for _ in range(2)]
        sq = P.sb([128, 128], F32, "sq")
        ss = [P.sb([128, 1], F32, "ss") for _ in range(2)]
        on = [P.sb([128, 128], F32, "on") for _ in range(2)]
        for h in range(8):
            kTh, vh, qTh = kbuf[h % 2], vbuf[h % 2], qbuf[h % 2]
            P.dma("sp", kTh.v, kTs[h])
            P.dma("pool", vh.rearrange("p t e -> p (t e)"), vs[h])
            P.dma("sp", qTh.v, qTs[h])
            slope = 2.0 ** (-(h + 1))
            it = 0
            for cmp in range(2):
                for qc in range(NT // 512):
                    accs = [P.psum[4 + tt] for tt in range(4)]
                    for kb in range(S // 128):
                        ps = P.psum[kb % 4]
                        P.mm(ps.v, kTh[cmp * 64:(cmp + 1) * 64, kb * 128:(kb + 1) * 128],
                             qTh[cmp * 64:(cmp + 1) * 64, qc * 512:(qc + 1) * 512])
                        u0 = 512 * qc - 128 * kb + 3968
                        tmp = tmpb[it % 3]
                        pt = ptb[it % 3]
                        it += 1
                        P.stt("dve", tmp.v, dtab[:, u0:u0 + 512], -slope, ps.v, ALU.mult, ALU.add)
                        P.act(pt.v, tmp.v, AF.Exp)
                        for tt in range(4):
                            P.mm(accs[tt][:, 0:129], pt[:, tt * 128:(tt + 1) * 128], vh[:, kb, 0:129],
                                 start=(kb == 0), stop=(kb == S // 128 - 1))
                    for tt in range(4):
                        qt = qc * 4 + tt
                        r = rc[tt % 2]
                        P.recip(r.v, accs[tt][:, 128:129])
                        P.ts("dve", oc[:, cmp, qt, :], accs[tt][:, 0:128], r.v, None, ALU.mult)
            for qt in range(NT // 128):
                o_ = od[qt % 2]
                s_ = ss[qt % 2]
                n_ = on[qt % 2]
                P.stt("dve", o_.v, oc[:, 1, qt, :], nlam.v, oc[:, 0, qt, :], ALU.mult, ALU.add)
                P.memset("dve", s_.v, 0.0)
                P.act(sq.v, o_.v, AF.Square, accum_out=s_.v)
                P.act(s_.v, s_.v, AF.Sqrt, bias=LN_EPS, scale=1.0 / 128.0)
                P.recip(s_.v, s_.v)
                P.stt("dve", n_.v, o_.v, s_.v, sl.v, ALU.mult, ALU.mult)
                P.dma("sp" if qt % 2 else "pool", o_s[qt * 128:(qt + 1) * 128, h * 128:(h + 1) * 128], n_.v)


DIL = ((128, 1), (512, 4), (2048, 16))


def _chunks(lo, hi, step):
    out = []
    a = lo
    while a < hi:
        out.append((a, min(step, hi - a)))
        a += step
    return out


def dil_phase(P, c, xctx_rows, kvalid, w_qkv, mt_dram, o_s, scr):
    geo = []
    for (W, d) in DIL:
        nU = NT // d
        J = nU + 128
        u_lo = 1024 // d - 64
        geo.append((d, nU, J, u_lo, 0, J))
    import os
    with P.phase():
      if os.environ.get('DIL_SKIP1') != '1':
            c.tp_banks = [6, 7]
            c.tp_i = 0
            c.wstage = [P.sb([128, 2048], F32, "wstage") for _ in range(2)]
            c.ws_i = 0
            xT = P.sb([128, KC, S], BF16, "xTfull")
            load_xT(P, c, xctx_rows, S // 128, xT)
            wblk = P.sb([128, KC, 512], BF16, "wblk")
            zt = P.sb([128, 1056], BF16, "zt")
            P.memset("pool", zt.v, 0.0)
            vst = [P.sb([128, 16, 66], BF16, "vst") for _ in range(3)]
            vm = [P.sb([128, 1], F32, "vm") for _ in range(3)]
            ones162 = P.sb([128, 16, 2], F32, "ones162")
            P.memset("pool", ones162.v, 1.0)
            kvoff = [0, 2176, 2176 + 2560]
            kst = [P.sb([128, 4096], BF16, "kst") for _ in range(2)]
            wvb = [P.sb([128, KC, 512], BF16, "wvb") for _ in range(2)]
            ki = 0
            vi = 0
            for g, (d, nU, J, u_lo, jv0, jv1) in enumerate(geo):
                base = g * 3 * D
                kTd, qTd, vd = scr["kTd"][g], scr["qTd"][g], scr["vd"][g]
                for r in range(d):
                    if jv0 > 0:
                        P.dma("sp", vd[r * J:r * J + jv0, :], zt[0:jv0, :])
                    if jv1 < J:
                        P.dma("sp", vd[r * J + jv1:(r + 1) * J, :], zt[0:J - jv1, :])
                for blk in range(2):
                    load_w_bf16(P, c, wvb[blk].v, w_qkv[:, base + 2 * D + blk * 512:base + 2 * D + (blk + 1) * 512], KC, 512)
                for r in range(d):
                    for (j0, m) in _chunks(jv0, jv1, 128):
                        t0 = (u_lo + j0) * d + r
                        vs_ = vst[vi % 3]
                        vm_ = vm[vi % 3]
                        vi += 1
                        P.dma("sp", vm_[0:m, :], kvalid[kvoff[g] + r * J + j0:kvoff[g] + r * J + j0 + m, :])
                        P.ts("pool", vs_[0:m, :, 64:66], ones162[0:m, :, :], vm_[0:m, 0:1], None, ALU.mult)
                        for blk in range(2):
                            ps = P.psum[(vi + blk) % 4]
                            wv = wvb[blk]
                            for k in range(KC):
                                P.mm(ps[0:m, :], xT[:, k, t0:t0 + (m - 1) * d + 1:d], wv[:, k, :], start=(k == 0), stop=(k == KC - 1))
                            P.copy(evac_engine(c), vs_[0:m, 8 * blk:8 * blk + 8, 0:64], ps[0:m, :].rearrange("p (h e) -> p h e", h=8))
                        P.dma("sp" if vi % 2 else "pool", vd[r * J + j0:r * J + j0 + m, :], vs_[0:m, :, :].rearrange("p h e -> p (h e)"))
                for which in ("k", "q"):
                    cb = base + (D if which == "k" else 0)
                    for blk in range(2):
                        load_w_bf16(P, c, wblk.v, w_qkv[:, cb + blk * 512:cb + (blk + 1) * 512], KC, 512)
                        for p4 in range(4):
                            p = blk * 4 + p4
                            kb = kst[ki % 2]
                            ki += 1
                            if which == "k":
                                P.memset("pool", kb[:, 0:d * J], 0.0)
                            pi = 0
                            for r in range(d):
                                rng = _chunks(jv0, jv1, 512) if which == "k" else _chunks(64, 64 + nU, 512)
                                for (j0, n) in rng:
                                    t0 = (u_lo + j0) * d + r
                                    ps = P.psum[pi % 4]
                                    pi += 1
                                    for k in range(KC):
                                        P.mm(ps[:, 0:n], wblk[:, k, p4 * 128:(p4 + 1) * 128], xT[:, k, t0:t0 + (n - 1) * d + 1:d],
                                             start=(k == 0), stop=(k == KC - 1))
                                    if which == "k":
                                        P.copy(evac_engine(c), kb[:, r * J + j0:r * J + j0 + n], ps[:, 0:n])
                                    else:
                                        P.act(kb[:, r * nU + j0 - 64:r * nU + j0 - 64 + n], ps[:, 0:n], AF.Identity, scale=0.125)
                            if which == "k":
                                P.dma("sp" if p % 2 else "pool", kTd[p], kb[:, 0:d * J])
                            else:
                                P.dma("sp" if p % 2 else "pool", qTd[p], kb[:, 0:NT])
    import os
    if os.environ.get('DIL_STOP') == '1':
        return
    with P.phase():
        mt = P.sb([128, 2, 128], F32, "mt")
        P.dma("sp", mt.v, mt_dram.v)
        mb = P.sb([128, 2, 16, 128], F32, "mb")
        kTg = P.sb([128, 8, 4096], BF16, "kTg")
        qTg = P.sb([128, 8, NT], BF16, "qTg")
        vt = [P.sb([128, 2, 1056], BF16, "vt") for _ in range(2)]
        tmpb = [P.sb([128, 512], F32, "tmpb") for _ in range(8)]
        ptb = [P.sb([128, 512], BF16, "ptb") for _ in range(8)]
        nb = [P.sb([128, 16, 65], F32, "nb") for _ in range(2)]
        it = 0
        qi = 0
        for g, (d, nU, J, u_lo, jv0, jv1) in enumerate(geo):
            kTd, qTd, vd, num = scr["kTd"][g], scr["qTd"][g], scr["vd"][g], scr["num"][g]
            for kt in range(2):
                for h in range(16):
                    P.ts("pool", mb[:, kt, h, :], mt[:, kt, :], -(2.0 ** (-(h + 1) / 2.0)) * d, None, ALU.mult)
            for p in range(8):
                P.dma("sp" if p % 2 else "pool", kTg[:, p, 0:d * J], kTd[p])
                P.dma("pool" if p % 2 else "sp", qTg[:, p, :], qTd[p])
            cut = os.environ.get('DIL_CUT', '')
            if cut == 'mb' or (cut in ('1g', '1q') and g > 0):
                continue
            for r in range(d):
                for qb in range(nU // 128):
                    if cut == '1q' and (qb > 0 or r > 0 or g > 0):
                        continue
                    v_ = vt[qi % 2]
                    n_ = nb[qi % 2]
                    qi += 1
                    P.dma("sp", v_.v, vd[r * J + qb * 128:r * J + qb * 128 + 256, :].rearrange("(t p) c -> p t c", p=128))
                    for hb8 in range(2):
                        for i in range(8):
                            h = hb8 * 8 + i
                            p, hp, slot = h // 2, h % 2, i // 2
                            for kt in range(2):
                                kc0 = r * J + qb * 128 + kt * 128
                                qc0 = r * nU + qb * 128
                                P.mm(P.psum[hp * 2 + kt][:, slot * 128:(slot + 1) * 128],
                                     kTg[hp * 64:(hp + 1) * 64, p, kc0:kc0 + 128], qTg[hp * 64:(hp + 1) * 64, p, qc0:qc0 + 128])
                        pts = {}
                        for hp in range(2):
                            for kt in range(2):
                                tmp = tmpb[it % 8]
                                pt = ptb[it % 8]
                                it += 1
                                P.tt("dve", tmp.rearrange("p (z q) -> p z q", z=4), P.psum[hp * 2 + kt].rearrange("p (z q) -> p z q", z=4),
                                     mb[:, kt, hb8 * 8 + hp:hb8 * 8 + 8:2, :], ALU.add)
                                P.act(pt.v, tmp.v, AF.Exp)
                                pts[(hp, kt)] = pt
                        accs = [P.psum[4 + hb8 * 2], P.psum[5 + hb8 * 2]]
                        for i in range(8):
                            h = hb8 * 8 + i
                            hp, slot = h % 2, i // 2
                            for kt in range(2):
                                P.mm(accs[i // 4][:, (i % 4) * 128:(i % 4) * 128 + 65], pts[(hp, kt)][:, slot * 128:(slot + 1) * 128],
                                     v_[:, kt, h * 66:h * 66 + 65], start=(kt == 0), stop=(kt == 1))
                        for j in range(2):
                            P.copy(evac_engine(c), n_[:, hb8 * 8 + 4 * j:hb8 * 8 + 4 * j + 4, :],
                                   accs[j].rearrange("p (h e) -> p h e", h=4)[:, :, 0:65])
                    t0 = qb * 128 * d + r
                    dst = num[t0:t0 + 127 * d + 1:d, :]
                    if cut != 'noout':
                        P.dma("pool", dst, n_.rearrange("p h e -> p (h e)"))
    if os.environ.get('DIL_STOP') == '2':
        return
    with P.phase():
        nbs = [[P.sb([128, 16, 65], F32, "cn") for _ in range(3)] for _ in range(2)]
        rcp = [P.sb([128, 16, 1], F32, "rcp") for _ in range(2)]
        ob = [P.sb([128, 16, 64], F32, "ob") for _ in range(2)]
        for t in range(NT // 128):
            a, b_, c_ = nbs[t % 2]
            for g, x_ in enumerate((a, b_, c_)):
                P.dma("sp" if g % 2 else "pool", x_.rearrange("p h e -> p (h e)"), scr["num"][g][t * 128:(t + 1) * 128, :])
            P.tt("dve", a.v, a.v, b_.v, ALU.add)
            P.tt("dve", a.v, a.v, c_.v, ALU.add)
            rc = rcp[t % 2]
            P.recip(rc.v, a[:, :, 64:65])
            o_ = ob[t % 2]
            P.tt("dve", o_.v, a[:, :, 0:64], rc.v.to_broadcast([128, 16, 64]), ALU.mult)
            P.dma("sp", o_s[t * 128:(t + 1) * 128, :], o_.rearrange("p h e -> p (h e)"))


TC = 8
TV = 128
HB = 16
CW = 256


def bc_head(v16):
    return v16.to_broadcast([128, 16, 64])


def rwkv_pre_phase(P, c, xpad, Wd, scr):
    with P.phase():
        c.tp_banks = [6, 7]
        c.tp_i = 0
        wr_ = P.sb([128, KC, D], BF16, "w_r")
        wk_ = P.sb([128, KC, D], BF16, "w_k")
        wv_ = P.sb([128, KC, D], BF16, "w_v")
        w1_ = P.sb([128, KC, 64], BF16, "w1")
        a1_ = P.sb([128, KC, 64], BF16, "a1")
        g1_ = P.sb([128, KC, 128], BF16, "g1")
        w2_ = P.sb([64, 1, D], BF16, "w2")
        a2_ = P.sb([64, 1, D], BF16, "a2")
        g2_ = P.sb([128, 1, D], BF16, "g2")
        w0r = P.sb([1, D], BF16, "w0r")
        a0r = P.sb([1, D], BF16, "a0r")
        ones1 = P.sb([1, 128], BF16, "ones1")
        with P.phase():
            c.wstage = [P.sb([128, 2048], F32, "wstage") for _ in range(2)]
            c.ws_i = 0
            load_w_bf16(P, c, wr_.v, Wd["w_r"].v, KC, D)
            load_w_bf16(P, c, wk_.v, Wd["w_k"].v, KC, D)
            load_w_bf16(P, c, wv_.v, Wd["w_v"].v, KC, D)
            load_w_bf16(P, c, w1_.v, Wd["w1"].v, KC, 64)
            load_w_bf16(P, c, a1_.v, Wd["a1"].v, KC, 64)
            load_w_bf16(P, c, g1_.v, Wd["g1"].v, KC, 128)
            load_w_bf16(P, c, g2_.v, Wd["g2"].v, 1, D)
            for dst, src in ((w2_, Wd["w2"]), (a2_, Wd["a2"])):
                st = c.wstage[c.ws_i % 2]
                c.ws_i += 1
                P.dma("pool", st[0:64, 0:D], src.v)
                P.copy("pool", dst[0:64, 0, :], st[0:64, 0:D])
            for dst, src in ((w0r, Wd["w0"]), (a0r, Wd["a0"])):
                st = c.wstage[c.ws_i % 2]
                c.ws_i += 1
                P.dma("pool", st[0:1, 0:D], src.v)
                P.copy("pool", dst.v, st[0:1, 0:D])
            P.memset("pool", ones1.v, 1.0)
        mu = P.sb([128, 6, KC], F32, "mu")
        P.dma("sp", mu.v, Wd["mu"].v)
        kkb = P.sb([128, D], F32, "kkb")
        kab = P.sb([128, D], F32, "kab")
        rkb = P.sb([128, D], F32, "rkb")
        P.dma("sp", kkb.v, Wd["k_k"].v.partition_broadcast(128))
        P.dma("sp", kab.v, Wd["k_a"].v.partition_broadcast(128))
        P.dma("sp", rkb.v, Wd["r_k"].v.partition_broadcast(128))
        xTc = P.sb([128, KC, CW + 128], F32, "xTc")
        xstage = [P.sb([128, D], F32, "xstage") for _ in range(2)]
        dk = [P.sb([128, CW], F32, "dk") for _ in range(2)]
        xm = P.sb([128, 6, KC, CW], BF16, "xm")
        hwT = P.sb([64, CW], BF16, "hwT")
        haT = P.sb([64, CW], BF16, "haT")
        hgT = P.sb([128, CW], BF16, "hgT")
        tm = {n: P.sb([128, D], F32, n) for n in ("r", "k", "v", "w", "a", "g", "kk", "t1", "kp", "kka")}
        n2 = P.sb([128, 16], F32, "n2")
        cst = P.sb([128, 48], F32, "cst")
        fm = {n: [P.sb([128, KC, 128], F32, "fm_" + n) for _ in range(2)] for n in ("nkk", "wr", "w")}
        ti = 0
        for ch in range(S // CW):
            t0 = ch * CW
            load_xT(P, c, xpad[PAD + t0 - 128:PAD + t0 + CW, :], (CW + 128) // 128, xTc, 0, xstage=xstage)
            for k in range(KC):
                d_ = dk[k % 2]
                P.tt("dve", d_.v, xTc[:, k, 127:127 + CW], xTc[:, k, 128:128 + CW], ALU.subtract)
                for j in range(6):
                    P.stt("dve", xm[:, j, k, :], d_.v, mu[:, j, k:k + 1], xTc[:, k, 128:128 + CW],
                          ALU.mult, ALU.add)
            ph = P.psum[4]
            for k in range(KC):
                P.mm(ph[0:64, 0:CW], w1_[:, k, :], xm[:, 1, k, :], start=(k == 0), stop=(k == KC - 1))
            P.act(hwT.v, ph[0:64, 0:CW], AF.Tanh)
            ph = P.psum[5]
            for k in range(KC):
                P.mm(ph[0:64, 0:CW], a1_[:, k, :], xm[:, 4, k, :], start=(k == 0), stop=(k == KC - 1))
            P.copy("dve", haT.v, ph[0:64, 0:CW])
            ph = P.psum[4]
            for k in range(KC):
                P.mm(ph[:, 0:CW], g1_[:, k, :], xm[:, 5, k, :], start=(k == 0), stop=(k == KC - 1))
            P.act(hgT.v, ph[:, 0:CW], AF.Sigmoid)
            for tl in range(CW // 128):
                tok = slice(tl * 128, (tl + 1) * 128)
                row = t0 + tl * 128
                for name, j, wmat in (("r", 0, wr_), ("k", 2, wk_), ("v", 3, wv_)):
                    for cc in range(2):
                        ps = P.psum[cc]
                        for k in range(KC):
                            P.mm(ps.v, xm[:, j, k, tok], wmat[:, k, cc * 512:(cc + 1) * 512], start=(k == 0), stop=(k == KC - 1))
                        P.copy(evac_engine(c), tm[name][:, cc * 512:(cc + 1) * 512], ps.v)
                for cc in range(2):
                    cs = slice(cc * 512, (cc + 1) * 512)
                    ps = P.psum[2]
                    P.mm(ps.v, hwT[0:64, tok], w2_[0:64, 0, cs], start=True, stop=False)
                    P.mm(ps.v, ones1[0:1, :], w0r[0:1, cs], start=False, stop=True)
                    P.act(tm["w"][:, cs], ps.v, AF.Sigmoid)
                    ps = P.psum[3]
                    P.mm(ps.v, haT[0:64, tok], a2_[0:64, 0, cs], start=True, stop=False)
                    P.mm(ps.v, ones1[0:1, :], a0r[0:1, cs], start=False, stop=True)
                    P.act(tm["a"][:, cs], ps.v, AF.Sigmoid)
                    ps = P.psum[cc]
                    P.mm(ps.v, hgT[:, tok], g2_[:, 0, cs])
                    P.copy(evac_engine(c), tm["g"][:, cs], ps.v)
                P.act(tm["w"].v, tm["w"].v, AF.Exp, scale=-math.exp(-0.5))
                P.dma("sp", scr["g"][row:row + 128, :], tm["g"].v)
                P.dma("pool", scr["v"][row:row + 128, :], tm["v"].v)
                r3, k3, v3 = (tm[n].rearrange("p (h e) -> p h e", h=16) for n in ("r", "k", "v"))
                kk, t1, kp, kka = tm["kk"], tm["t1"], tm["kp"], tm["kka"]
                kk3, t13, kp3, kka3 = (x.rearrange("p (h e) -> p h e", h=16) for x in (kk, t1, kp, kka))
                P.tt("pool", kk.v, tm["k"].v, kkb.v, ALU.mult)
                P.tt("dve", t1.v, kk.v, kk.v, ALU.mult)
                P.reduce("dve", n2.v, t13, ALU.add)
                P.act(n2.v, n2.v, AF.Sqrt)
                P.ts("dve", n2.v, n2.v, 1e-12, None, ALU.max)
                P.recip(n2.v, n2.v)
                P.tt("dve", kk3, kk3, bc_head(n2.rearrange("p (h o) -> p h o", o=1)), ALU.mult)
                P.ts("pool", t1.v, tm["a"].v, -1.0, None, ALU.add)
                P.tt("pool", t1.v, t1.v, kab.v, ALU.mult)
                P.ts("pool", t1.v, t1.v, 1.0, None, ALU.add)
                P.tt("pool", kp.v, tm["k"].v, t1.v, ALU.mult)
                P.dma("sp", scr["kp"][row:row + 128, :], kp.v)
                P.tt("dve", kka.v, kk.v, tm["a"].v, ALU.mult)
                P.dma("pool", scr["kka"][row:row + 128, :], kka.v)
                P.tt("dve", t1.v, kka.v, tm["r"].v, ALU.mult)
                P.reduce("dve", cst[:, 0:16], t13, ALU.add)
                P.tt("dve", t1.v, kp.v, tm["r"].v, ALU.mult)
                P.reduce("dve", cst[:, 16:32], t13, ALU.add)
                P.tt("dve", t1.v, t1.v, rkb.v, ALU.mult)
                P.reduce("dve", cst[:, 32:48], t13, ALU.add)
                P.dma("sp", scr["cst"][row:row + 128, :], cst.v)
                P.ts("pool", kka.v, kk.v, -1.0, None, ALU.mult)
                P.tt("pool", t1.v, tm["w"].v, tm["r"].v, ALU.mult)
                for name, src in (("nkk", kka), ("wr", t1), ("w", tm["w"])):
                    f_ = fm[name][ti % 2]
                    transpose_tile(P, c, src, f_, 0)
                    P.dma("sp" if name != "wr" else "pool",
                          scr[name + "T"][:, row:row + 128].rearrange("(k p) t -> p k t", p=128), f_.v)
                ti += 1


def rwkv_scan_phase(P, c, scr, nsteps=S):
    with P.phase():
        ST = [P.sb([128, 4, 64], F32, "ST") for _ in range(2)]
        T1 = [P.sb([128, 4, 64], F32, "T1") for _ in range(2)]
        for s_ in ST:
            P.memset("dve", s_.v, 0.0)
        VEC = [P.sb([128, 8, 4, TV], F32, "VEC") for _ in range(2)]
        WT = [P.sb([128, 8, TV], F32, "WT") for _ in range(2)]
        for v_ in VEC:
            P.memset("pool", v_.v, 0.0)
        Bm = [[P.sb([6, TC, 4, 128], F32, "Bm") for _ in range(2)] for _ in range(2)]
        Am = [[P.sb([6, TC, 4, 64], F32, "Am") for _ in range(2)] for _ in range(2)]
        for bb in Bm:
            for b_ in bb:
                P.memset("pool", b_.v, 0.0)
        for aa in Am:
            for a_ in aa:
                P.memset("pool", a_.v, 0.0)
        nkkT = scr["nkkT"].rearrange("(k p) t -> p k t", p=128)
        wrT = scr["wrT"].rearrange("(k p) t -> p k t", p=128)
        wTd = scr["wT"].rearrange("(k p) t -> p k t", p=128)

        def tok3(name, a, b, hc):
            return scr[name][a:b, :].rearrange("t (p x) -> t p x", x=128)[:, hc * 4:hc * 4 + 4, :]

        for t in range(nsteps):
            tv, tc = t % TV, t % TC
            if tv == 0:
                vb, wb = VEC[(t // TV) % 2], WT[(t // TV) % 2]
                P.dma("sp", vb[0:64, :, 0, :], nkkT[0:64, :, t:t + TV])
                P.dma("sp", vb[64:128, :, 1, :], nkkT[64:128, :, t:t + TV])
                P.dma("sp", vb[0:64, :, 2, :], wrT[0:64, :, t:t + TV])
                P.dma("sp", vb[64:128, :, 3, :], wrT[64:128, :, t:t + TV])
                P.dma("sp", wb.v, wTd[:, :, t:t + TV])
            if tc == 0:
                bms, ams = Bm[(t // TC) % 2], Am[(t // TC) % 2]
                for hc in range(2):
                    bm, am = bms[hc], ams[hc]
                    kk_ = tok3("kka", t, t + TC, hc)
                    kp_ = tok3("kp", t, t + TC, hc)
                    v_ = tok3("v", t, t + TC, hc)
                    P.dma("sp", bm[0:1, :, :, 0:64], kk_[:, :, 0:64].unsqueeze(0))
                    P.dma("sp", bm[1:2, :, :, 64:128], kk_[:, :, 64:128].unsqueeze(0))
                    P.dma("sp", bm[4:5, :, :, 0:64], kp_[:, :, 0:64].unsqueeze(0))
                    P.dma("sp", bm[5:6, :, :, 64:128], kp_[:, :, 64:128].unsqueeze(0))
                    P.dma("sp", am[4:5, :, :, :], v_[:, :, 0:64].unsqueeze(0))
                    P.dma("sp", am[5:6, :, :, :], v_[:, :, 64:128].unsqueeze(0))
            for hc in range(2):
                P.tt("pool", T1[hc].v, ST[hc].v, wb[:, hc * 4:hc * 4 + 4, tv:tv + 1].to_broadcast([128, 4, 64]), ALU.mult)
            for hc in range(2):
                for pp in range(4):
                    P.mm(P.psum[hc * 2][0:4, pp * 64:(pp + 1) * 64], vb[:, hc * 4 + pp, :, tv], ST[hc][:, pp, :])
            for hc in range(2):
                P.copy("act", ams[hc][0:4, tc, :, :], P.psum[hc * 2][0:4, 0:256].rearrange("c (p i) -> c p i", p=4))
            for hc in range(2):
                for pp in range(4):
                    P.mm(P.psum[hc * 2 + 1][:, pp * 64:(pp + 1) * 64], bms[hc][0:6, tc, pp, :], ams[hc][0:6, tc, pp, :])
            for hc in range(2):
                P.tt("dve", ST[hc].v, T1[hc].v, P.psum[hc * 2 + 1][:, 0:256].rearrange("q (p i) -> q p i", p=4), ALU.add)
            if tc == TC - 1:
                tb = t - (TC - 1)
                for hc in range(2):
                    am = ams[hc]
                    sa_ = tok3("sa", tb, tb + TC, hc)
                    yp_ = tok3("yp", tb, tb + TC, hc)
                    P.dma("pool", sa_[:, :, 0:64].unsqueeze(0), am[0:1, :, :, :])
                    P.dma("pool", sa_[:, :, 64:128].unsqueeze(0), am[1:2, :, :, :])
                    P.dma("pool", yp_[:, :, 0:64].unsqueeze(0), am[2:3, :, :, :])
                    P.dma("pool", yp_[:, :, 64:128].unsqueeze(0), am[3:4, :, :, :])


def rwkv_post_phase(P, c, scr, gn_g_dram, gn_b_dram, y_out):
    with P.phase():
        gng = P.sb([128, D], F32, "gng")
        gnb = P.sb([128, D], F32, "gnb")
        P.dma("sp", gng.v, gn_g_dram.partition_broadcast(128))
        P.dma("sp", gnb.v, gn_b_dram.partition_broadcast(128))
        bufs = [{n: P.sb([128, D], F32, n) for n in ("sa", "yp", "v", "g", "y", "t")} for _ in range(2)]
        csts = [P.sb([128, 48], F32, "cst") for _ in range(2)]
        mv = [P.sb([128, 16], F32, "mean") for _ in range(2)]
        vr = [P.sb([128, 16], F32, "var") for _ in range(2)]
        for t in range(S // 128):
            b_ = bufs[t % 2]
            cs = csts[t % 2]
            rows = slice(t * 128, (t + 1) * 128)
            P.dma("sp", b_["sa"].v, scr["sa"][rows, :])
            P.dma("pool", b_["yp"].v, scr["yp"][rows, :])
            P.dma("sp", b_["v"].v, scr["v"][rows, :])
            P.dma("pool", b_["g"].v, scr["g"][rows, :])
            P.dma("sp", cs.v, scr["cst"][rows, :])
            sa3, yp3, v3, g3, y3, t3 = (b_[n].rearrange("p (h e) -> p h e", h=16) for n in ("sa", "yp", "v", "g", "y", "t"))

            def hb(lo):
                return bc_head(cs[:, lo:lo + 16].rearrange("p (h o) -> p h o", o=1))
            P.tt("dve", t3, sa3, hb(0), ALU.mult)
            P.tt("dve", y3, yp3, t3, ALU.add)
            P.tt("pool", t3, v3, hb(16), ALU.mult)
            P.tt("dve", y3, y3, t3, ALU.add)
            m_, v_ = mv[t % 2], vr[t % 2]
            P.reduce("dve", m_.v, y3, ALU.add)
            P.ts("dve", m_.v, m_.v, 1.0 / 64.0, None, ALU.mult)
            P.tt("dve", y3, y3, bc_head(m_.rearrange("p (h o) -> p h o", o=1)), ALU.subtract)
            P.tt("pool", t3, y3, y3, ALU.mult)
            P.reduce("dve", v_.v, t3, ALU.add)
            P.act(v_.v, v_.v, AF.Sqrt, bias=GN_EPS, scale=1.0 / 64.0)
            P.recip(v_.v, v_.v)
            P.tt("dve", y3, y3, bc_head(v_.rearrange("p (h o) -> p h o", o=1)), ALU.mult)
            P.tt("pool", b_["y"].v, b_["y"].v, gng.v, ALU.mult)
            P.tt("pool", b_["y"].v, b_["y"].v, gnb.v, ALU.add)
            P.tt("dve", t3, v3, hb(32), ALU.mult)
            P.tt("dve", y3, y3, t3, ALU.add)
            P.tt("pool", b_["y"].v, b_["y"].v, b_["g"].v, ALU.mult)
            P.dma("sp", y_out[rows, :], b_["y"].v)

from concourse.bass_utils import run_bass_kernel_spmd

NCORES = 8
_PROGS = {}
_DBG = {}


def _ext(P, name, shape, dtype=F32):
    return P.dram(name, shape, dtype, kind="ExternalInput", disjoint=False)


def _xatt_inputs(P):
    return dict(mem=_ext(P, "mem", [MEM, D]), xw_q=_ext(P, "xw_q", [D, D]), xw_kv=_ext(P, "xw_kv", [D, 2 * D]),
                xw_o=_ext(P, "xw_o", [D, D]))


def build_A(kind):
    P = Prog()
    c = setup_common(P, _ext(P, "ident", [128, 128]))
    xfull = _ext(P, "xfull", [S, D])
    xown = _ext(P, "xown", [NT, D])
    w_o = _ext(P, "w_o", [D, D])
    lg = _ext(P, "lg", [2, D])
    lb = _ext(P, "lb", [2, D])
    xi = _xatt_inputs(P)
    o_s = P.dram("o_s", [NT, D])
    x1 = P.dram("x1", [NT, D])
    x2 = P.dram("x2", [NT, D], kind="ExternalOutput")
    if kind == "diff":
        w_qkv = _ext(P, "w_qkv", [D, 3 * D])
        lam = _ext(P, "lam", [1, 256])
        subln = _ext(P, "subln", [1, 128])
        dtab = _ext(P, "dtab", [128, 8192])
        linit = _ext(P, "linit", [1, 2])
        scr = {"kTs": P.dram("kTs", [8, 128, S], BF16), "vs": P.dram("vs", [8, 128, 32 * 130], BF16),
               "qTs": P.dram("qTs", [8, 128, NT], BF16)}
        diff_phase(P, c, xfull.v, xown.v, w_qkv, lam.v, subln.v, dtab, linit.v, o_s, scr)
    else:
        w_qkv = _ext(P, "w_qkv", [D, 9 * D])
        kvalid = _ext(P, "kvalid", [8832, 1])
        mt = _ext(P, "mt", [128, 2, 128])
        scr = {"kTd": [], "qTd": [], "vd": [], "num": []}
        for g, (W, d) in enumerate(DIL):
            J = NT // d + 128
            scr["kTd"].append(P.dram(f"kTd{g}", [8, 128, d * J], BF16))
            scr["qTd"].append(P.dram(f"qTd{g}", [8, 128, NT], BF16))
            scr["vd"].append(P.dram(f"vd{g}", [d * J, 1056], BF16))
            scr["num"].append(P.dram(f"num{g}", [NT, 1040], F32))
        dil_phase(P, c, xfull.v, kvalid, w_qkv, mt, o_s, scr)
    proj_ln_phase(P, c, o_s.v, xown.v, w_o, lg[0:1, :], lb[0:1, :], x1.v)
    xatt_phase(P, c, x1.v, x2.v, xi["mem"], xi["xw_q"], xi["xw_kv"], xi["xw_o"], lg[1:2, :], lb[1:2, :])
    P.finish()
    return P


def build_F():
    P = Prog()
    c = setup_common(P, _ext(P, "ident", [128, 128]))
    x2h = _ext(P, "x2h", [NT + 256, D])
    w_in = _ext(P, "w_in", [D, 2 * DFF])
    w_out = _ext(P, "w_out", [DFF, D])
    convw = _ext(P, "convw", [128, FC, 3])
    convb = _ext(P, "convb", [128, FC])
    lg = _ext(P, "lg", [1, D])
    lb = _ext(P, "lb", [1, D])
    x3 = P.dram("x3", [NT, D], kind="ExternalOutput")
    ffn_phase(P, c, x2h, 128, x3.v, w_in, w_out, convw, convb, lg.v, lb.v)
    P.finish()
    return P


RW_SHAPES = (("mu", [128, 6, KC]), ("w_r", [D, D]), ("w_k", [D, D]), ("w_v", [D, D]), ("w1", [D, 64]), ("w2", [64, D]),
             ("a1", [D, 64]), ("a2", [64, D]), ("g1", [D, 128]), ("g2", [128, D]), ("w0", [1, D]), ("a0", [1, D]),
             ("k_k", [1, D]), ("k_a", [1, D]), ("r_k", [1, D]), ("gn_g", [1, D]), ("gn_b", [1, D]))


def build_R1():
    P = Prog()
    c = setup_common(P, _ext(P, "ident", [128, 128]))
    xpad = _ext(P, "xpad", [SP_, D])
    Wd = {n: _ext(P, "W_" + n, shp) for n, shp in RW_SHAPES}
    scr = {n: P.dram("s_" + n, [S, D], F32) for n in ("kka", "kp", "v", "g", "sa", "yp")}
    scr.update({n: P.dram("s_" + n, [D, S], F32) for n in ("nkkT", "wrT", "wT")})
    scr["cst"] = P.dram("s_cst", [S, 48], F32)
    y = P.dram("y", [S, D], F32, kind="ExternalOutput")
    rwkv_pre_phase(P, c, xpad, Wd, scr)
    rwkv_scan_phase(P, c, scr)
    rwkv_post_phase(P, c, scr, Wd["gn_g"].v, Wd["gn_b"].v, y)
    P.finish()
    return P


def build_R2():
    P = Prog()
    c = setup_common(P, _ext(P, "ident", [128, 128]))
    yf = _ext(P, "yf", [NT, D])
    yb = _ext(P, "yb", [NT, D])
    xown = _ext(P, "xown", [NT, D])
    w_o = _ext(P, "w_o", [D, D])
    lg = _ext(P, "lg", [2, D])
    lb = _ext(P, "lb", [2, D])
    xi = _xatt_inputs(P)
    x1 = P.dram("x1", [NT, D])
    x2 = P.dram("x2", [NT, D], kind="ExternalOutput")
    proj_ln_phase(P, c, yf.v, xown.v, w_o, lg[0:1, :], lb[0:1, :], x1.v, src2_rows=yb.v)
    xatt_phase(P, c, x1.v, x2.v, xi["mem"], xi["xw_q"], xi["xw_kv"], xi["xw_o"], lg[1:2, :], lb[1:2, :])
    P.finish()
    return P


def _prog(name, fn, *a):
    if name not in _PROGS:
        _PROGS[name] = fn(*a)
    return _PROGS[name]


def _run(P, in_maps):
    res = run_bass_kernel_spmd(P.nc, in_maps, core_ids=list(range(NCORES)))
    return res.results


def _c(a):
    return np.ascontiguousarray(a, dtype=np.float32)


def _rows(xb, lo, hi):
    out = np.zeros((hi - lo, xb.shape[1]), np.float32)
    a, b = max(lo, 0), min(hi, xb.shape[0])
    out[a - lo:b - lo] = xb[a:b]
    return out


def kernel(**inp):
    x = _c(inp["x"])
    mem = _c(inp["mem"])
    ident = np.eye(128, dtype=np.float32)
    ki = np.arange(128)[:, None]
    qj = np.arange(128)[None, :]
    BIG = 30000.0
    mt = _c(np.stack([np.where(ki >= qj, np.abs(ki - qj - 64), BIG), np.where(ki <= qj, np.abs(ki - qj + 64), BIG)], 1))
    uu = np.arange(8192)[None, :]
    dtabs = [_c(np.abs(ki - (uu - 3968) - own)) for own in (0, NT)]
    cur = x
    for i in range(4):
        m, j = i % 3, i // 3
        xat = dict(xw_q=_c(inp["xatt_w_q"][i]), xw_kv=_c(inp["xatt_w_kv"][i]), xw_o=_c(inp["xatt_w_o"][i]))
        lg, lb = _c(inp["ln_g"][i]), _c(inp["ln_b"][i])
        if m in (0, 1):
            ims = []
            for cid in range(NCORES):
                b, h = cid // 2, cid % 2
                own = h * NT
                im = dict(ident=ident, xown=_c(cur[b, own:own + NT]), lg=_c(lg[0:2]), lb=_c(lb[0:2]), mem=mem[b], **xat)
                if m == 0:
                    li = 0.8 - 0.6 * math.exp(-0.3 * i)
                    im.update(xfull=_c(cur[b]), w_qkv=_c(inp["diff_w_qkv"][j]), w_o=_c(inp["diff_w_o"][j]),
                              lam=_c(inp["diff_lambda"][j].reshape(1, 256)), subln=_c(inp["diff_subln"][j][None]),
                              dtab=dtabs[h], linit=np.array([[-li, 1.0 - li]], np.float32))
                else:
                    kvt = _rows(np.ones((S, 1), np.float32), own - 1024, own + 3072)[:, 0]
                    kv = []
                    for (W_, d_) in DIL:
                        J_ = NT // d_ + 128
                        tok = ((1024 // d_ - 64 + np.arange(J_))[None, :] * d_ + np.arange(d_)[:, None]).reshape(-1)
                        kv.append(kvt[tok])
                    kv = _c(np.concatenate(kv)[:, None])
                    im.update(xfull=_rows(cur[b], own - 1024, own + 3072), kvalid=kv, w_qkv=_c(inp["dil_w_qkv"][j]),
                              w_o=_c(inp["dil_w_o"][j]), mt=mt)
                ims.append(im)
            res = _run(_prog("A_diff" if m == 0 else "A_dil", build_A, "diff" if m == 0 else "dil"), ims)
        else:
            g = lambda n: inp["rwkv_" + n][j]
            ims = []
            for cid in range(NCORES):
                b, dr = cid // 2, cid % 2
                xb = cur[b][::-1] if dr else cur[b]
                xpad = np.zeros((SP_, D), np.float32)
                xpad[PAD:PAD + S] = xb
                ims.append({"ident": ident, "xpad": xpad,
                            "W_mu": _c(g("mu")[dr].reshape(6, KC, 128).transpose(2, 0, 1)),
                            "W_w_r": _c(g("w_rkv")[0]), "W_w_k": _c(g("w_rkv")[1]), "W_w_v": _c(g("w_rkv")[2]),
                            "W_w1": _c(g("w1")[dr]), "W_w2": _c(g("w2")[dr]), "W_a1": _c(g("a1")[dr]), "W_a2": _c(g("a2")[dr]),
                            "W_g1": _c(g("g1")[dr]), "W_g2": _c(g("g2")[dr]), "W_w0": _c(g("w0")[dr][None]),
                            "W_a0": _c(g("a0")[dr][None]), "W_k_k": _c(g("k_k")[dr][None]), "W_k_a": _c(g("k_a")[dr][None]),
                            "W_r_k": _c(g("r_k").reshape(1, D)), "W_gn_g": _c(g("gn_g")[None]), "W_gn_b": _c(g("gn_b")[None])})
            r1 = _run(_prog("R1", build_R1), ims)
            ims = []
            for cid in range(NCORES):
                b, h = cid // 2, cid % 2
                own = h * NT
                yf = r1[2 * b]["y"]
                yb = r1[2 * b + 1]["y"][::-1]
                ims.append(dict(ident=ident, yf=_c(yf[own:own + NT]), yb=_c(yb[own:own + NT]), xown=_c(cur[b, own:own + NT]),
                                w_o=_c(g("w_o")), lg=_c(lg[0:2]), lb=_c(lb[0:2]), mem=mem[b], **xat))
            res = _run(_prog("R2", build_R2), ims)
        x2 = np.stack([np.concatenate([res[2 * b]["x2"], res[2 * b + 1]["x2"]], 0) for b in range(NB)], 0)
        _DBG[f"x2_{i}"] = x2
        cw = _c(inp["ffn_conv_w"][i].reshape(3, FC, 128).transpose(2, 1, 0))
        cb = _c(inp["ffn_conv_b"][i].reshape(FC, 128).T)
        ims = []
        for cid in range(NCORES):
            b, h = cid // 2, cid % 2
            own = h * NT
            ims.append(dict(ident=ident, x2h=_rows(x2[b], own - 128, own + NT + 128), w_in=_c(inp["ffn_w_in"][i]),
                            w_out=_c(inp["ffn_w_out"][i]), convw=cw, convb=cb, lg=_c(lg[2:3]), lb=_c(lb[2:3])))
        res = _run(_prog("F", build_F), ims)
        cur = np.stack([np.concatenate([res[2 * b]["x3"], res[2 * b + 1]["x3"]], 0) for b in range(NB)], 0)
        _DBG[f"x3_{i}"] = cur
    return cur.astype(np.float32)
```
